# Optimizing a Trainium2 kernel written in Bass

```python
import jax, jax.numpy as jnp
from jax import lax
import numpy as np

D_MODEL = 1024
BATCH = 4
SEQ = 8192
DEPTH = 2

N_ATTN_HEADS = 8
ATTN_HEAD_DIM = 64
ATTN_WIDTH = N_ATTN_HEADS * ATTN_HEAD_DIM
N_REC_HEADS = 4
REC_HEAD_DIM = 128
REC_WIDTH = N_REC_HEADS * REC_HEAD_DIM
D_MIX = ATTN_WIDTH + REC_WIDTH
IN_COLS = 3 * ATTN_WIDTH + N_ATTN_HEADS + 4 * REC_WIDTH
SPLIT_POINTS = (ATTN_WIDTH, 2 * ATTN_WIDTH, 3 * ATTN_WIDTH,
                3 * ATTN_WIDTH + N_ATTN_HEADS,
                3 * ATTN_WIDTH + N_ATTN_HEADS + REC_WIDTH,
                3 * ATTN_WIDTH + N_ATTN_HEADS + 2 * REC_WIDTH,
                3 * ATTN_WIDTH + N_ATTN_HEADS + 3 * REC_WIDTH)
D_FF = 2816
CONV_WIDTH = 3
FOX_BLOCK = 128
HGRN_CHUNK = 64
LN_EPS = 1e-5
RMS_EPS = 1e-6
DEEPNORM_ALPHA = (2.0 * DEPTH) ** 0.25
DEEPNORM_BETA = (8.0 * DEPTH) ** -0.25

kernel_name = 'hybrid_fox_hgrn2_deepnorm'


def _layer_norm(x, g, b):
    xf = x.astype(jnp.float32)
    mu = jnp.mean(xf, axis=-1, keepdims=True)
    var = jnp.mean(jnp.square(xf - mu), axis=-1, keepdims=True)
    y = (xf - mu) * lax.rsqrt(var + LN_EPS)
    return (y * g.astype(jnp.float32) + b.astype(jnp.float32)).astype(x.dtype)


def _rms_norm_heads(x, g):
    xf = x.astype(jnp.float32)
    y = xf * lax.rsqrt(jnp.mean(jnp.square(xf), axis=-1, keepdims=True) + RMS_EPS)
    return y * g.astype(jnp.float32)


def _fox_attention(q, k, v, logf):
    B, H, S, D = q.shape
    n_blocks = S // FOX_BLOCK
    c = jnp.cumsum(logf.astype(jnp.float32), axis=-1)
    qf = q.astype(jnp.float32) * (D ** -0.5)
    kf = k.astype(jnp.float32)
    vf = v.astype(jnp.float32)
    q_blocks = qf.reshape(B, H, n_blocks, FOX_BLOCK, D).transpose(2, 0, 1, 3, 4)
    c_blocks = c.reshape(B, H, n_blocks, FOX_BLOCK).transpose(2, 0, 1, 3)
    key_pos = jnp.arange(S)

    def block(args):
        q_i, c_i, i = args
        s = jnp.einsum('bhqd,bhkd->bhqk', q_i, kf) + c_i[..., :, None] - c[..., None, :]
        q_pos = i * FOX_BLOCK + jnp.arange(FOX_BLOCK)
        s = jnp.where(key_pos[None, :] <= q_pos[:, None], s, -jnp.inf)
        p = jax.nn.softmax(s, axis=-1)
        return jnp.einsum('bhqk,bhkd->bhqd', p, vf)

    o = lax.map(block, (q_blocks, c_blocks, jnp.arange(n_blocks)))
    return o.transpose(1, 2, 0, 3, 4).reshape(B, H, S, D)


def _hgrn2_chunked(q, k, v, logf):
    B, H, S, DK = q.shape
    DV = v.shape[-1]
    C = HGRN_CHUNK
    n_chunks = S // C

    def to_chunks(t):
        return t.astype(jnp.float32).reshape(B, H, n_chunks, C, t.shape[-1]).transpose(2, 0, 1, 3, 4)

    qc, kc, vc, gc = to_chunks(q), to_chunks(k), to_chunks(v), to_chunks(logf)
    b = jnp.cumsum(gc, axis=-2)
    causal = jnp.tril(jnp.ones((C, C), dtype=bool))

    def step(state, inp):
        q_i, k_i, v_i, b_i = inp
        diff = b_i[..., :, None, :] - b_i[..., None, :, :]
        decay = jnp.where(causal[:, :, None], jnp.exp(jnp.minimum(diff, 0.0)), 0.0)
        scores = jnp.einsum('bhtd,bhsd,bhtsd->bhts', q_i, k_i, decay)
        o = jnp.einsum('bhts,bhse->bhte', scores, v_i) \
            + jnp.einsum('bhtd,bhde->bhte', q_i * jnp.exp(b_i), state)
        b_last = b_i[..., -1:, :]
        state = jnp.exp(b_last[..., 0, :])[..., None] * state \
            + jnp.einsum('bhsd,bhse->bhde', k_i * jnp.exp(b_last - b_i), v_i)
        return state, o

    state0 = jnp.zeros((B, H, DK, DV), jnp.float32)
    _, o = lax.scan(step, state0, (qc, kc, vc, b))
    return o.transpose(1, 2, 0, 3, 4).reshape(B, H, S, DV)


def _causal_depthwise_conv(h, w, bias):
    S = h.shape[1]
    hp = jnp.pad(h, ((0, 0), (CONV_WIDTH - 1, 0), (0, 0)))
    y = bias
    for j in range(CONV_WIDTH):
        y = y + w[j] * hp[:, j:j + S, :]
    return y


def setup_inputs(seed: int = 0) -> dict:
    key = jax.random.key(seed)
    ks = jax.random.split(key, 17)

    def nrm(k, shape, scale):
        return jax.random.normal(k, shape, jnp.float32) * scale

    col_scale = jnp.concatenate([
        jnp.ones((2 * ATTN_WIDTH,), jnp.float32),
        jnp.full((ATTN_WIDTH,), DEEPNORM_BETA, jnp.float32),
        jnp.ones((N_ATTN_HEADS + 2 * REC_WIDTH,), jnp.float32),
        jnp.full((REC_WIDTH,), DEEPNORM_BETA, jnp.float32),
        jnp.ones((REC_WIDTH,), jnp.float32)])

    return {
        'x': nrm(ks[0], (BATCH, SEQ, D_MODEL), 1.0),
        'ln_emb_g': 1.0 + nrm(ks[1], (D_MODEL,), 0.05),
        'ln_emb_b': nrm(ks[2], (D_MODEL,), 0.02),
        'w_in': nrm(ks[3], (DEPTH, D_MODEL, IN_COLS), D_MODEL ** -0.5) * col_scale,
        'fox_f_bias': 1.0 + nrm(ks[4], (DEPTH, N_ATTN_HEADS), 0.1),
        'fox_norm_g': 1.0 + nrm(ks[5], (DEPTH, ATTN_WIDTH), 0.05),
        'hgrn_lower_bounds': nrm(ks[6], (DEPTH, REC_WIDTH), 0.1),
        'hgrn_norm_g': 1.0 + nrm(ks[7], (DEPTH, REC_WIDTH), 0.05),
        'w_o': nrm(ks[8], (DEPTH, D_MIX, D_MODEL), D_MIX ** -0.5 * DEEPNORM_BETA),
        'ln_mix_g': 1.0 + nrm(ks[9], (DEPTH, D_MODEL), 0.05),
        'ln_mix_b': nrm(ks[10], (DEPTH, D_MODEL), 0.02),
        'w_up': nrm(ks[11], (DEPTH, D_MODEL, 2 * D_FF), D_MODEL ** -0.5 * DEEPNORM_BETA),
        'conv_w': nrm(ks[12], (DEPTH, CONV_WIDTH, 2 * D_FF), CONV_WIDTH ** -0.5),
        'conv_b': nrm(ks[13], (DEPTH, 2 * D_FF), 0.02),
        'w_down': nrm(ks[14], (DEPTH, D_FF, D_MODEL), D_FF ** -0.5 * DEEPNORM_BETA),
        'ln_ffn_g': 1.0 + nrm(ks[15], (DEPTH, D_MODEL), 0.05),
        'ln_ffn_b': nrm(ks[16], (DEPTH, D_MODEL), 0.02),
    }


def reference(x, ln_emb_g, ln_emb_b, w_in, fox_f_bias, fox_norm_g, hgrn_lower_bounds,
              hgrn_norm_g, w_o, ln_mix_g, ln_mix_b, w_up, conv_w, conv_b, w_down,
              ln_ffn_g, ln_ffn_b):
    B, S, _ = x.shape
    dtype = x.dtype

    lb_sm = jax.nn.softmax(hgrn_lower_bounds.astype(jnp.float32), axis=0)
    lb_cum = jnp.cumsum(lb_sm, axis=0)
    lower_bounds = lb_cum - lb_cum[0:1]

    def heads(t, n_heads):
        return t.reshape(B, S, n_heads, -1).transpose(0, 2, 1, 3)

    x = _layer_norm(x, ln_emb_g, ln_emb_b)

    for l in range(DEPTH):
        proj = x @ w_in[l]
        q_a, k_a, v_a, f_a, q_r, f_r, i_r, g_r = jnp.split(proj, SPLIT_POINTS, axis=-1)

        logf_a = jax.nn.log_sigmoid((f_a + fox_f_bias[l]).astype(jnp.float32)).transpose(0, 2, 1)
        o_a = _fox_attention(heads(q_a, N_ATTN_HEADS), heads(k_a, N_ATTN_HEADS),
                             heads(v_a, N_ATTN_HEADS), logf_a)
        o_a = _rms_norm_heads(o_a.transpose(0, 2, 1, 3),
                              fox_norm_g[l].reshape(N_ATTN_HEADS, ATTN_HEAD_DIM))
        o_a = o_a.reshape(B, S, ATTN_WIDTH)

        lb = lower_bounds[l]
        logf_r = jnp.logaddexp(jnp.log(lb),
                               jnp.log1p(-lb) + jax.nn.log_sigmoid(f_r.astype(jnp.float32)))
        k_r = -jnp.expm1(logf_r)
        o_r = _hgrn2_chunked(heads(jax.nn.silu(q_r), N_REC_HEADS), heads(k_r, N_REC_HEADS),
                             heads(i_r, N_REC_HEADS), heads(logf_r, N_REC_HEADS))
        o_r = _rms_norm_heads(o_r.transpose(0, 2, 1, 3),
                              hgrn_norm_g[l].reshape(N_REC_HEADS, REC_HEAD_DIM))
        o_r = o_r.reshape(B, S, REC_WIDTH) * jax.nn.silu(g_r.astype(jnp.float32))

        mix = jnp.concatenate([o_a, o_r], axis=-1).astype(dtype) @ w_o[l]
        x = _layer_norm(DEEPNORM_ALPHA * x + mix, ln_mix_g[l], ln_mix_b[l])

        h = _causal_depthwise_conv(x @ w_up[l], conv_w[l], conv_b[l])
        a, u = jnp.split(h, 2, axis=-1)
        ffn = (jax.nn.gelu(a, approximate=False) * u) @ w_down[l]
        x = _layer_norm(DEEPNORM_ALPHA * x + ffn, ln_ffn_g[l], ln_ffn_b[l])

    return x
```

```python
import numpy as np
import ml_dtypes
from contextlib import ExitStack

import concourse.bass as bass
import concourse.mybir as mybir
from concourse.bass_utils import run_bass_kernel_spmd

F32 = mybir.dt.float32
BF16 = mybir.dt.bfloat16
AF = mybir.ActivationFunctionType
ALU = mybir.AluOpType
AX = mybir.AxisListType
NPBF = ml_dtypes.bfloat16

D = 1024
S = 8192
NB = 4
DEPTH = 2
TOK = 4096
DFF = 2816
INC = 3592
ALPHA = (2.0 * DEPTH) ** 0.25
LN_EPS = 1e-5
RMS_EPS = 1e-6
NEG = -30000.0

ENG_ATTR = {"pe": "tensor", "act": "scalar", "dve": "vector", "pool": "gpsimd", "sp": "sync"}
SAFE_SAME_ENGINE = True


class _Op:
    __slots__ = ("eng", "name", "args", "kw", "reads", "writes", "chan", "deps", "inc", "sig")

    def __init__(self, eng, name, args, kw, reads, writes, chan):
        self.eng, self.name, self.args, self.kw = eng, name, args, kw
        self.reads, self.writes, self.chan = tuple(reads), tuple(writes), chan
        self.deps = ()
        self.inc = False
        self.sig = 0


class Prog:
    def __init__(self, nc):
        self.nc = nc
        self.sems = {}
        self.semval = {}

    def sem(self, key):
        if key not in self.sems:
            self.sems[key] = self.nc.alloc_semaphore(name="s_%s" % (str(key).replace(" ", "")))
            self.semval[key] = 0
        return self.sems[key]


class Phase:
    def __init__(self, prog, sched=True):
        self.prog = prog
        self.ops = []
        self.chan_ids = {}
        self.sched = sched

    def add(self, eng, name, *args, reads=(), writes=(), chan=None, **kw):
        if chan is not None:
            if chan not in self.chan_ids:
                self.chan_ids[chan] = len(self.chan_ids)
            chan = ("dma", self.chan_ids[chan])
        self.ops.append(_Op(eng, name, args, kw, reads, writes, chan))

    @staticmethod
    def _dur(op):
        def free(ap):
            try:
                sh = list(ap.shape)
                n = 1
                for d in sh[1:]:
                    n *= int(d)
                return n
            except Exception:
                return 512
        if op.chan is not None:
            return 0.06
        a = op.args
        if op.eng == "pe":
            n = free(a[2]) if len(a) > 2 else 128
            return max(64, n) / 2400.0 + 0.02
        n = free(a[0]) if a else 64
        if op.eng == "pool":
            return 0.12 + 2.4 * n / 960.0
        if op.eng == "dve" and op.name in ("tensor_tensor", "tensor_tensor_scan"):
            return 0.07 + 2.0 * n / 960.0
        return 0.07 + n / 960.0

    def reorder(self):
        import heapq
        ops = self.ops
        n = len(ops)
        last_w, readers, last_chan = {}, {}, {}
        preds = [None] * n
        for i, op in enumerate(ops):
            deps = set()
            for r in op.reads:
                if r in last_w:
                    deps.add(last_w[r])
            for w in op.writes:
                if w in last_w:
                    deps.add(last_w[w])
                deps.update(readers.get(w, ()))
            if op.chan is not None:
                if op.chan in last_chan:
                    deps.add(last_chan[op.chan])
                last_chan[op.chan] = i
            deps.discard(i)
            preds[i] = deps
            for w in op.writes:
                last_w[w] = i
                readers[w] = []
            for r in op.reads:
                if r not in op.writes:
                    readers.setdefault(r, []).append(i)
        succ = [[] for _ in range(n)]
        indeg = [0] * n
        for i in range(n):
            indeg[i] = len(preds[i])
            for d in preds[i]:
                succ[d].append(i)
        ready_t = [0.0] * n
        fin = [0.0] * n
        t_eng = {}
        heap = [(0.0, i) for i in range(n) if indeg[i] == 0]
        heapq.heapify(heap)
        order = []
        while heap:
            rt, i = heapq.heappop(heap)
            op = ops[i]
            st = max(rt, t_eng.get(op.eng, 0.0))
            d = self._dur(op)
            t_eng[op.eng] = st + d
            if op.chan is not None:
                fin[i] = st + 2.5
            else:
                fin[i] = st + d
            order.append(i)
            for j in succ[i]:
                lat = 0.0 if (ops[j].eng == op.eng and op.chan is None) else 0.25
                ready_t[j] = max(ready_t[j], fin[i] + lat)
                indeg[j] -= 1
                if indeg[j] == 0:
                    heapq.heappush(heap, (ready_t[j], j))
        assert len(order) == n
        self.ops = [ops[i] for i in order]

    def emit(self):
        import os as _os
        if self.sched and _os.environ.get("K_SCHED", "1") == "1":
            self.reorder()
        prog, ops = self.prog, self.ops
        nc = prog.nc
        last_w, readers, last_chan = {}, {}, {}
        for i, op in enumerate(ops):
            deps = set()
            for r in op.reads:
                if r in last_w:
                    deps.add(last_w[r])
            for w in op.writes:
                if w in last_w:
                    deps.add(last_w[w])
                deps.update(readers.get(w, ()))
            if op.chan is not None and op.chan in last_chan:
                deps.add(last_chan[op.chan])
            if op.chan is not None:
                last_chan[op.chan] = i
            deps.discard(i)
            best = {}
            for d in deps:
                p = ops[d]
                key = p.chan if p.chan is not None else p.eng
                if p.chan is None and p.eng == op.eng and op.chan is None:
                    if p.eng == "pe" or not SAFE_SAME_ENGINE:
                        continue
                if key not in best or best[key] < d:
                    best[key] = d
            op.deps = tuple(best.values())
            for d in op.deps:
                ops[d].inc = True
            for w in op.writes:
                last_w[w] = i
                readers[w] = []
            for r in op.reads:
                if r not in op.writes:
                    readers.setdefault(r, []).append(i)
        for op in ops:
            if op.chan is not None:
                prog.sem(op.chan)
                prog.semval[op.chan] += 16
                op.sig = prog.semval[op.chan]
            elif op.inc:
                prog.sem(op.eng)
                prog.semval[op.eng] += 1
                op.sig = prog.semval[op.eng]
        final_dma = {k: prog.semval[k] for k in prog.semval if isinstance(k, tuple)}

        def body(ename):
            def run(eng):
                waited = {}
                for op in ops:
                    if op.eng != ename:
                        continue
                    for d in op.deps:
                        p = ops[d]
                        key = p.chan if p.chan is not None else p.eng
                        if waited.get(key, -1) >= p.sig:
                            continue
                        eng.wait_ge(prog.sems[key], p.sig)
                        waited[key] = p.sig
                    ins = getattr(eng, op.name)(*op.args, **op.kw)
                    if op.chan is not None:
                        ins.then_inc(prog.sems[op.chan], 16)
                    elif op.inc:
                        ins.then_inc(prog.sems[op.eng], 1)
                if ename == "sp":
                    for k, v in final_dma.items():
                        if v > 0 and waited.get(k, -1) < v:
                            eng.wait_ge(prog.sems[k], v)
            return run

        with nc.Block() as block:
            for ename, attr in ENG_ATTR.items():
                getattr(block, attr)(body(ename))
        self.ops = []


class Ring:
    def __init__(self, name, tiles):
        self.name, self.tiles, self.i = name, tiles, 0

    def next(self):
        k = self.i % len(self.tiles)
        self.i += 1
        return self.tiles[k], (self.name, k)


class Ctx:
    def __init__(self):
        self.nc = bass.Bass("TRN2", target_bir_lowering=False)
        self.prog = Prog(self.nc)
        self.inputs = []
        self.outputs = []

    def din(self, name, shape, dt=F32):
        self.inputs.append(name)
        return self.nc.dram_tensor(name, list(shape), dt, kind="ExternalInput").ap()

    def dout(self, name, shape, dt=F32):
        self.outputs.append(name)
        return self.nc.dram_tensor(name, list(shape), dt, kind="ExternalOutput").ap()

    def dscr(self, name, shape, dt=F32):
        return self.nc.dram_tensor(name, list(shape), dt).ap()


class Alloc:
    counter = [0]

    def __init__(self, nc):
        self.nc = nc
        self.st = ExitStack()

    def sb(self, shape, dt=F32, name=None):
        Alloc.counter[0] += 1
        return self.st.enter_context(self.nc.sbuf_tensor("%s_%d" % (name or "sb", Alloc.counter[0]), list(shape), dt))

    def ps(self, shape, dt=F32, name=None):
        Alloc.counter[0] += 1
        return self.st.enter_context(self.nc.psum_tensor("%s_%d" % (name or "ps", Alloc.counter[0]), list(shape), dt))

    def close(self):
        self.st.close()


def make_ident(ph, al):
    ones = al.sb([128, 128], F32, "ones")
    idf = al.sb([128, 128], F32, "idf")
    idb = al.sb([128, 128], BF16, "idb")
    ph.add("pool", "memset", ones[:], 1.0, writes=[("ones",)])
    ph.add("pool", "affine_select", idf[:], ones[:], [[-1, 128]], ALU.is_equal, 0.0,
           base=0, channel_multiplier=1, reads=[("ones",)], writes=[("idf",)])
    ph.add("pool", "tensor_copy", idb[:], idf[:], reads=[("idf",)], writes=[("ident",)])
    return idb


WCH = 1024


def load_weight_bf16(ph, al, w_ap, K, N, name, stage):
    kc_n = K // 128
    wt = al.sb([128, kc_n, N], BF16, name)
    wv = w_ap.rearrange("(kc p) n -> p kc n", p=128)
    engs = ["act", "dve"]
    j = 0
    for c0 in range(0, N, WCH):
        for kc in range(kc_n):
            cw = min(WCH, N - c0)
            st, sr = stage.next()
            ph.add("sp", "dma_start", out=st[:, 0:cw], in_=wv[:, kc, c0:c0 + cw], chan=sr, writes=[sr])
            e = engs[j % 2]
            j += 1
            if e == "act":
                ph.add("act", "activation", wt[:, kc, c0:c0 + cw], st[:, 0:cw], AF.Copy,
                       reads=[sr], writes=[(name, kc, c0)])
            else:
                ph.add(e, "tensor_copy", wt[:, kc, c0:c0 + cw], st[:, 0:cw], reads=[sr], writes=[(name, kc, c0)])
    return wt


def wcols(name, kc, c_lo, c_hi):
    return [(name, kc, c0) for c0 in range((c_lo // WCH) * WCH, c_hi, WCH)]


def bcast_load(ph, al, ap, n, name, P=128):
    t = al.sb([P, n], F32, name)
    ph.add("sp", "dma_start", out=t[:], in_=ap.partition_broadcast(P), chan=name, writes=[(name,)])
    return t


def wres(name, kc_n, N, c_lo, c_hi):
    out = []
    for kc in range(kc_n):
        out += wcols(name, kc, c_lo, min(c_hi, N))
    return out


def layer_norm(ph, P, y, yr, xn, xnr, gB, bB, tmp, tag):
    k_, tmp = tmp.next()
    tag = (tag, k_)
    st, mv, rs = tmp["st"], tmp["mv"], tmp["rs"]
    r_st, r_mv, r_rs = (tag, "st"), (tag, "mv"), (tag, "rs")
    ph.add("dve", "bn_stats", st[0:P, 0:6], y[0:P, 0:512], reads=[yr], writes=[r_st])
    ph.add("dve", "bn_stats", st[0:P, 6:12], y[0:P, 512:1024], reads=[yr, r_st], writes=[r_st])
    ph.add("dve", "bn_aggr", mv[0:P, :], st[0:P, :], reads=[r_st], writes=[r_mv])
    ph.add("dve", "tensor_scalar", rs[0:P, :], mv[0:P, 1:2], LN_EPS, None, ALU.add, reads=[r_mv], writes=[r_rs])
    ph.add("act", "activation", rs[0:P, :], rs[0:P, :], AF.Sqrt, reads=[r_rs], writes=[r_rs])
    ph.add("dve", "reciprocal", rs[0:P, :], rs[0:P, :], reads=[r_rs], writes=[r_rs])
    nm = tmp["nm"]
    ph.add("dve", "tensor_scalar", nm[0:P, :], mv[0:P, 0:1], rs[0:P, 0:1], -1.0, ALU.mult, ALU.mult,
           reads=[r_mv, r_rs], writes=[(tag, "nm")])
    ph.add("act", "activation", xn[0:P, :], y[0:P, :], AF.Identity, bias=nm[0:P, 0:1], scale=rs[0:P, 0:1],
           reads=[yr, r_rs, (tag, "nm")], writes=[xnr])
    ph.add("pool", "tensor_tensor", xn[0:P, :], xn[0:P, :], gB[0:P, :], ALU.mult, reads=[xnr, ("lnc",)], writes=[xnr])
    ph.add("pool", "tensor_tensor", xn[0:P, :], xn[0:P, :], bB[0:P, :], ALU.add, reads=[xnr, ("lnc",)], writes=[xnr])


class _LnTmp:
    def __init__(self, al, n=3):
        self.sets = [{"st": al.sb([128, 12], F32, "lnst"), "mv": al.sb([128, 2], F32, "lnmv"), "rs": al.sb([128, 1], F32, "lnrs"),
                      "nm": al.sb([128, 1], F32, "lnnm")} for _ in range(n)]
        self.i = 0

    def next(self):
        k = self.i % len(self.sets)
        self.i += 1
        return k, self.sets[k]


def ln_tmp(al):
    return _LnTmp(al)


def transpose_to(ph, P, xb, xbr, psT, psr, dst, dstr, ident, evac_eng):
    for kc in range(8):
        ph.add("pe", "transpose", psT[:, kc * 128:kc * 128 + P], xb[0:P, kc * 128:(kc + 1) * 128], ident[0:P, 0:P],
               reads=[xbr, ("ident",)], writes=[psr])
    src = psT[:, :].rearrange("p (kc t) -> p kc t", kc=8)[:, :, 0:P]
    if evac_eng == "act":
        ph.add("act", "activation", dst, src, AF.Copy, reads=[psr], writes=[dstr])
    else:
        ph.add(evac_eng, "tensor_copy", dst, src, reads=[psr], writes=[dstr])


def phase_proj(cx, x_src, xres_dst, do_ln, lng, lnb, w_in, fbias, lbnd, layer, outs, ntok=TOK, caug=None, full_from=0):
    nc, prog = cx.nc, cx.prog
    ph = Phase(prog)
    al = Alloc(nc)
    ident = make_ident(ph, al)
    stage = Ring("stg", [al.sb([128, WCH], F32, "stg") for _ in range(4)])
    W = load_weight_bf16(ph, al, w_in, D, INC, "win", stage)
    gB = bB = None
    if do_ln:
        gB = al.sb([128, D], F32, "gB")
        bB = al.sb([128, D], F32, "bB")
        ph.add("sp", "dma_start", out=gB[:], in_=lng.partition_broadcast(128), chan="c0", writes=[("lnc0",)])
        ph.add("sp", "dma_start", out=bB[:], in_=lnb.partition_broadcast(128), chan="c1", writes=[("lnc1",)])
        ph.add("pool", "engine_nop", reads=[("lnc0",), ("lnc1",)], writes=[("lnc",)])
    nfb = al.sb([8, 1], F32, "nfb")
    ph.add("sp", "dma_start", out=nfb[:], in_=fbias.rearrange("(p o) -> p o", o=1), chan="c2", writes=[("nfb",)])
    ph.add("dve", "tensor_scalar", nfb[:], nfb[:], -1.0, None, ALU.mult, reads=[("nfb",)], writes=[("nfb",)])
    lbt = al.sb([128, 2, 4], F32, "lbt")
    lb = al.sb([128, 4], F32, "lb")
    oml = al.sb([128, 4], F32, "oml")
    ph.add("sp", "dma_start", out=lbt[:], in_=lbnd.rearrange("l (h p) -> p l h", p=128), chan="c3", writes=[("lbt",)],
           allow_slow_non_contiguous=True)
    if layer == 0:
        ph.add("pool", "memset", lb[:], 0.0, reads=[("lbt",)], writes=[("lb",)])
    else:
        ph.add("dve", "tensor_tensor", lb[:], lbt[:, 1, :], lbt[:, 0, :], ALU.subtract, reads=[("lbt",)], writes=[("lb",)])
        ph.add("act", "activation", lb[:], lb[:], AF.Sigmoid, reads=[("lb",)], writes=[("lb",)])
    ph.add("dve", "tensor_scalar", oml[:], lb[:], -1.0, 1.0, ALU.mult, ALU.add, reads=[("lb",)], writes=[("oml",)])

    xt = Ring("xt", [al.sb([128, D], F32, "xt") for _ in range(2)])
    xn = Ring("xn", [al.sb([128, D], F32, "xn") for _ in range(2)])
    xb = Ring("xb", [al.sb([128, D], BF16, "xb") for _ in range(2)])
    xT = Ring("xT", [al.sb([128, 8, 512], BF16, "xT") for _ in range(2)])
    tmp = ln_tmp(al)
    psT = Ring("psT", [al.ps([128, 1024], BF16, "psT") for _ in range(2)])
    psA = Ring("psA", [al.ps([128, 512], F32, "psA") for _ in range(6)])
    evb = Ring("evb", [al.sb([128, 512], BF16, "evb") for _ in range(4)])
    evf = Ring("evf", [al.sb([128, 512], F32, "evf") for _ in range(4)])
    evs = Ring("evs", [al.sb([128, 512], F32, "evs") for _ in range(2)])
    sm = Ring("sm", [al.sb([8, 512], F32, "sm") for _ in range(2)])

    qT, kT, vO, lfO, qsT, fTO, vrO, gtO = outs
    if caug is not None:
        ones8 = al.sb([8, 512], F32, "ones8")
        ph.add("pool", "memset", ones8[:], 1.0, writes=[("ones8",)])
        cR = Ring("cR", [al.sb([8, 512], F32, "cR") for _ in range(2)])
        r1c = al.sb([8, 512], F32, "r1c")
        t3R = Ring("t3R", [al.sb([8, 6, 512], BF16, "t3R") for _ in range(2)])
        cprev = None
    wall = wres("win", 8, INC, 0, INC)
    for g in range(ntok // 512):
        xTt, xTr = xT.next()
        for t in range(4):
            r0 = g * 512 + t * 128
            a, ar = xt.next()
            ph.add("sp", "dma_start", out=a[:], in_=x_src[r0:r0 + 128, :], chan=ar, writes=[ar])
            if do_ln:
                n, nr = xn.next()
                layer_norm(ph, 128, a, ar, n, nr, gB, bB, tmp, "ln")
                ph.add("sp", "dma_start", out=xres_dst[r0:r0 + 128, :], in_=n[:], chan=("st",) + nr, reads=[nr])
            else:
                n, nr = a, ar
            b, br = xb.next()
            ph.add("act", "activation", b[:], n[:], AF.Copy, reads=[nr], writes=[br])
            pt, ptr = psT.next()
            transpose_to(ph, 128, b, br, pt, ptr, xTt[:, :, t * 128:(t + 1) * 128], xTr + (t,), ident, "dve")
        xTall = [xTr + (t,) for t in range(4)]
        blocks = []
        for j in range(4):
            blocks.append(("q", 0 + j * 128, 128, j))
        for j in range(4):
            blocks.append(("k", 512 + j * 128, 128, j))
        blocks.append(("f", 1536, 8, 0))
        for j in range(4):
            blocks.append(("qr", 1544 + j * 128, 128, j))
        for j in range(4):
            blocks.append(("fr", 2056 + j * 128, 128, j))
        tsl = slice(g * 512, (g + 1) * 512)
        for kind, c0, cw, j in blocks:
            if g < full_from and kind in ("q", "qr"):
                continue
            ps, psr = psA.next()
            for kc in range(8):
                ph.add("pe", "matmul", ps[0:cw, :], W[:, kc, c0:c0 + cw], xTt[:, kc, :], start=(kc == 0), stop=(kc == 7),
                       reads=wcols("win", kc, c0, c0 + cw) + xTall, writes=[psr])
            if kind == "q":
                e, er = evb.next()
                ph.add("act", "activation", e[:], ps[:, :], AF.Copy, scale=0.125, reads=[psr], writes=[er])
                ph.add("sp", "dma_start", out=qT[j * 128:(j + 1) * 128, tsl], in_=e[:], chan=("st",) + er, reads=[er])
            elif kind == "k":
                e, er = evb.next()
                ph.add("dve", "tensor_copy", e[:], ps[:, :], reads=[psr], writes=[er])
                ph.add("sp", "dma_start", out=kT[j * 128:(j + 1) * 128, tsl], in_=e[:], chan=("st",) + er, reads=[er])
            elif kind == "f":
                e, er = sm.next()
                ph.add("act", "activation", e[:], ps[0:8, :], AF.Exp, bias=nfb[:, 0:1], scale=-1.0,
                       reads=[psr, ("nfb",)], writes=[er])
                ph.add("act", "activation", e[:], e[:], AF.Ln, bias=1.0, scale=1.0, reads=[er], writes=[er])
                ph.add("dve", "tensor_scalar", e[:], e[:], -1.0, None, ALU.mult, reads=[er], writes=[er])
                ph.add("sp", "dma_start", out=lfO[:, tsl], in_=e[:], chan=("st",) + er, reads=[er])
                if caug is not None:
                    c, cr = cR.next()
                    init = 0.0 if cprev is None else cprev[0][:, 511:512]
                    ph.add("dve", "tensor_tensor_scan", c[:], ones8[:], e[:], init, ALU.mult, ALU.add,
                           reads=[er, ("ones8",)] + ([] if cprev is None else [cprev[1]]), writes=[cr])
                    cprev = (c, cr)
                    t3, tr = t3R.next()
                    ph.add("dve", "tensor_copy", t3[:, 0, :], c[:], reads=[cr], writes=[tr])
                    ph.add("dve", "tensor_tensor", r1c[:], c[:], t3[:, 0, :], ALU.subtract, reads=[cr, tr], writes=[("r1c",)])
                    ph.add("dve", "tensor_copy", t3[:, 1, :], r1c[:], reads=[("r1c",)], writes=[tr])
                    ph.add("dve", "tensor_tensor", r1c[:], r1c[:], t3[:, 1, :], ALU.subtract, reads=[("r1c",), tr], writes=[("r1c",)])
                    ph.add("dve", "tensor_copy", t3[:, 2, :], r1c[:], reads=[("r1c",)], writes=[tr])
                    ph.add("dve", "tensor_scalar", t3[:, 3:6, :], t3[:, 0:3, :], -1.0, None, ALU.mult, reads=[tr], writes=[tr])
                    ph.add("sp", "dma_start", out=caug[:, :, tsl], in_=t3[:], chan=("st",) + tr, reads=[tr])
            elif kind == "qr":
                e, er = evf.next()
                ph.add("act", "activation", e[:], ps[:, :], AF.Silu, reads=[psr], writes=[er])
                ph.add("sp", "dma_start", out=qsT[j * 128:(j + 1) * 128, tsl], in_=e[:], chan=("st",) + er, reads=[er])
            elif kind == "fr":
                s_, sr_ = evs.next()
                e, er = evf.next()
                ph.add("act", "activation", s_[:], ps[:, :], AF.Sigmoid, reads=[psr], writes=[sr_])
                ph.add("dve", "tensor_scalar", e[:], s_[:], oml[:, j:j + 1], lb[:, j:j + 1], ALU.mult, ALU.add,
                       reads=[sr_, ("oml",), ("lb",)], writes=[er])
                ph.add("sp", "dma_start", out=fTO[j * 128:(j + 1) * 128, tsl], in_=e[:], chan=("st",) + er, reads=[er])
        for t in range(4):
            r0 = g * 512 + t * 128
            for kind, c0 in (("v", 1024), ("ir", 2568), ("g", 3080)):
                if g < full_from and kind == "g":
                    continue
                ps, psr = psA.next()
                for kc in range(8):
                    ph.add("pe", "matmul", ps[:, :], xTt[:, kc, t * 128:(t + 1) * 128], W[:, kc, c0:c0 + 512],
                           start=(kc == 0), stop=(kc == 7), reads=wcols("win", kc, c0, c0 + 512) + [xTr + (t,)], writes=[psr])
                if kind == "v":
                    e, er = evb.next()
                    ph.add("dve", "tensor_copy", e[:], ps[:, :], reads=[psr], writes=[er])
                    ph.add("sp", "dma_start", out=vO[r0:r0 + 128, :], in_=e[:], chan=("st",) + er, reads=[er])
                elif kind == "ir":
                    e, er = evb.next()
                    ph.add("dve", "tensor_copy", e[:], ps[:, :], reads=[psr], writes=[er])
                    ph.add("sp", "dma_start", out=vrO[r0:r0 + 128, :], in_=e[:], chan=("st",) + er, reads=[er])
                else:
                    e, er = evf.next()
                    ph.add("act", "activation", e[:], ps[:, :], AF.Silu, reads=[psr], writes=[er])
                    ph.add("sp", "dma_start", out=gtO[r0:r0 + 128, :], in_=e[:], chan=("st",) + er, reads=[er])
    ph.emit()
    al.close()


def proj_outputs(cx):
    return (cx.dout("qT", [512, TOK], BF16), cx.dout("kT", [512, TOK], BF16), cx.dout("v", [TOK, 512], BF16),
            cx.dout("lf", [8, TOK], F32), cx.dout("qsT", [512, TOK], F32), cx.dout("fT", [512, TOK], F32),
            cx.dout("vr", [TOK, 512], BF16), cx.dout("gate", [TOK, 512], F32))


def build_launch1():
    cx = Ctx()
    x = cx.din("x", [TOK, D])
    lng = cx.din("ln_g", [D])
    lnb = cx.din("ln_b", [D])
    w_in = cx.din("w_in", [D, INC])
    fb = cx.din("fbias", [8])
    lbnd = cx.din("lbnd", [2, 512])
    xres = cx.dout("xres", [TOK, D])
    outs = proj_outputs(cx)
    phase_proj(cx, x, xres, True, lng, lnb, w_in, fb, lbnd, 0, outs)
    return cx


def build_launch2():
    cx = Ctx()
    qT = cx.din("qT", [256, S], BF16)
    kT = cx.din("kT", [256, S], BF16)
    v = cx.din("v", [S, 256], BF16)
    lf = cx.din("lf", [4, S])
    qsT = cx.din("qsT", [256, S])
    fT = cx.din("fT", [256, S])
    vr = cx.din("vr", [S, 256], BF16)
    gate = cx.din("gate", [S, 256])
    gfox = cx.din("gfox", [256])
    ghg = cx.din("ghg", [256])
    mixa = cx.dout("mixa", [S, 256], BF16)
    mixr = cx.dout("mixr", [S, 256], BF16)
    caug = cx.dscr("caug", [4, 6, S], BF16)
    mixer_phase(cx, qT, kT, v, lf, qsT, fT, vr, gate, gfox, ghg, mixa, mixr, caug)
    return cx


def mixer_phase(cx, qT, kT, v, lf, qsT, fT, vr, gate, gfox, ghg, mixa, mixr, caug, flag=None, qb0=0, caug_ready=False, out0=0, mrow=None):
    nc, prog = cx.nc, cx.prog
    KR = 70 if flag is None else 71
    HALF = S // 2
    import os as _osm
    ph = Phase(prog, sched=_osm.environ.get('K_MSCHED', '1') == '1')
    al = Alloc(nc)
    ident = make_ident(ph, al)

    zt = al.sb([128, 260], BF16, "zt")
    ph.add("pool", "memset", zt[:], 0.0, writes=[("zt",)])
    zf = al.sb([128, 128], F32, "zf")
    ph.add("pool", "memset", zf[:], 0.0, writes=[("zf",)])
    maskf = al.sb([128, 128], F32, "maskf")
    maskT = al.sb([128, 128], BF16, "maskT")
    ph.add("pool", "affine_select", maskf[:], zf[:], [[1, 128]], ALU.is_ge, NEG, base=0, channel_multiplier=-1,
           reads=[("zf",)], writes=[("maskf",)])
    ph.add("pool", "tensor_copy", maskT[:], maskf[:], reads=[("maskf",)], writes=[("maskT",)])
    onesf = al.sb([128, 512], F32, "onesf")
    ph.add("pool", "memset", onesf[:], 1.0, writes=[("onesf",)])
    tril = al.sb([64, 64], F32, "tril")
    ph.add("pool", "affine_select", tril[:], onesf[0:64, 0:64], [[1, 64]], ALU.is_ge, 0.0, base=0, channel_multiplier=-1,
           reads=[("onesf",)], writes=[("tril",)])
    rmask = al.sb([128, 512], F32, "rmask")
    ph.add("pool", "memset", rmask[:], 1.0, writes=[("rmask",)])
    ph.add("pool", "memset", rmask[:, :].rearrange("p (c t) -> p c t", t=64)[:, :, 0:1], 0.0,
           reads=[("rmask",)], writes=[("rmask",)])
    gA = al.sb([128, 256], F32, "gA")
    ph.add("sp", "dma_start", out=gA[:], in_=gfox.partition_broadcast(128), chan="gA", writes=[("gA",)])
    gR = al.sb([64, 256], F32, "gR")
    ph.add("sp", "dma_start", out=gR[:], in_=ghg.partition_broadcast(64), chan="gR", writes=[("gR",)])

    psSn = al.ps([128, 512], F32, "psSn")
    if not caug_ready:
        Tri = al.sb([128, 128], F32, "Tri")
        ph.add("pool", "affine_select", Tri[:], onesf[:, 0:128], [[1, 128]], ALU.is_gt, 0.0, base=0, channel_multiplier=-1,
               reads=[("onesf",)], writes=[("Tri",)])
        for hb in range(1, 4):
            ph.add("pool", "affine_select", Tri[:, 32 * hb:32 * hb + 32], Tri[:, 32 * hb:32 * hb + 32], [[0, 32]], ALU.is_ge, 0.0,
                   base=-32 * hb, channel_multiplier=1, reads=[("Tri",)], writes=[("Tri",)])
        lfa = al.sb([128, 256], F32, "lfa")
        cl = al.sb([128, 256], F32, "cl")
        r1 = al.sb([128, 256], F32, "r1")
        off = al.sb([128, 2], F32, "off")
        t3 = al.sb([128, 6, 256], BF16, "t3")
        ph.add("sp", "dma_start", out=lfa[:], in_=lf.rearrange("h (g t) -> (h g) t", t=256), chan="lfa", writes=[("lfa",)])
        ph.add("dve", "tensor_tensor_scan", cl[:], onesf[:, 0:256], lfa[:], 0.0, ALU.mult, ALU.add,
               reads=[("lfa",), ("onesf",)], writes=[("cl",)])
        ph.add("pe", "matmul", psSn[:, 0:2], Tri[:, :], cl[:, 254:256], start=True, stop=True,
               reads=[("Tri",), ("cl",)], writes=[("psSn",)])
        ph.add("dve", "tensor_copy", off[:], psSn[:, 0:2], reads=[("psSn",)], writes=[("off",)])
        ph.add("dve", "tensor_scalar", cl[:], cl[:], off[:, 1:2], None, ALU.add, reads=[("cl",), ("off",)], writes=[("cl",)])
        ph.add("dve", "tensor_copy", t3[:, 0, :], cl[:], reads=[("cl",)], writes=[("t3",)])
        ph.add("dve", "tensor_tensor", r1[:], cl[:], t3[:, 0, :], ALU.subtract, reads=[("cl",), ("t3",)], writes=[("r1",)])
        ph.add("dve", "tensor_copy", t3[:, 1, :], r1[:], reads=[("r1",)], writes=[("t3",)])
        ph.add("dve", "tensor_tensor", r1[:], r1[:], t3[:, 1, :], ALU.subtract, reads=[("r1",), ("t3",)], writes=[("r1",)])
        ph.add("dve", "tensor_copy", t3[:, 2, :], r1[:], reads=[("r1",)], writes=[("t3",)])
        ph.add("dve", "tensor_scalar", t3[:, 3:6, :], t3[:, 0:3, :], -1.0, None, ALU.mult, reads=[("t3",)], writes=[("t3",)])
        for h_ in range(4):
            ph.add("sp", "dma_start", out=caug[h_].rearrange("r (g t) -> g r t", t=256), in_=t3[h_ * 32:(h_ + 1) * 32, :, :],
                   chan=("cg", h_), reads=[("t3",)], writes=[("caug", h_)])
    caug_all = [] if caug_ready else [("caug", h_) for h_ in range(4)]

    Vt = al.sb([128, 64, 4, 65], BF16, "Vt")
    ph.add("pool", "memset", Vt[:], 1.0, writes=[("Vm",)])
    v4 = v.rearrange("(blk p) (h e) -> p blk h e", p=128, e=64)
    for h_ in range(4):
        for i in range(4):
            ph.add("sp", "dma_start", out=Vt[:, i * 16:(i + 1) * 16, h_, 0:64], in_=v4[:, i * 16:(i + 1) * 16, h_, :],
                   chan=("V", i), reads=[("Vm",)], writes=[("V", h_, i)])

    QA = [al.sb([128, S], BF16, "QA") for _ in range(2)]
    KA = [al.sb([128, S], BF16, "KA") for _ in range(2)]
    for k in range(2):
        ph.add("pool", "memset", QA[k][64:70, :], 1.0, writes=[("QA", k)])
        ph.add("pool", "memset", KA[k][64:70, :], 1.0, writes=[("KA", k)])
    if flag is not None:
        flagt = bcast_load(ph, al, flag, 1, "flagm")
        mval = al.sb([128, 1], F32, "mval")
        ph.add("dve", "tensor_scalar", mval[:], flagt[:], -1.0, -NEG, ALU.add, ALU.mult, reads=[("flagm",)], writes=[("mval",)])
        mt_ = al.sb([128, 2, 64], BF16, "mrowt")
        ph.add("pool", "memset", mt_[0:64, 0, :], 0.0, writes=[("mrowt", 0)])
        ph.add("pool", "memset", mt_[64:128, 0, :], 1.0, writes=[("mrowt", 1)])
        ph.add("pool", "memset", mt_[64:128, 1, :], 0.0, writes=[("mrowt", 2)])
        ph.add("dve", "tensor_scalar", mt_[0:64, 1, :], onesf[0:64, 0:64], mval[0:64, 0:1], None, ALU.mult,
               reads=[("onesf",), ("mval",)], writes=[("mrowt", 3)])
        ph.add("sp", "dma_start", out=mrow[0].rearrange("(p t) -> p t", t=64), in_=mt_[:, 0, :], chan=("mr", 0),
               reads=[("mrowt", 0), ("mrowt", 1)], writes=[("mrow", 0)])
        ph.add("sp", "dma_start", out=mrow[1].rearrange("(p t) -> p t", t=64), in_=mt_[:, 1, :], chan=("mr", 1),
               reads=[("mrowt", 2), ("mrowt", 3)], writes=[("mrow", 1)])
        for k in range(2):
            ph.add("sp", "dma_start", out=QA[k][70:71, :], in_=mrow[0:1, :], chan=("qm", k), reads=[("mrow", 0), ("QA", k)],
                   writes=[("QA", k)])
            ph.add("sp", "dma_start", out=KA[k][70:71, :], in_=mrow[1:2, :], chan=("km", k), reads=[("mrow", 1), ("KA", k)],
                   writes=[("KA", k)])
    PT = Ring("PT", [al.sb([128, 512], BF16, "PT") for _ in range(4)])
    psS = Ring("psS", [al.ps([128, 512], F32, "psS") for _ in range(3)])
    psO = Ring("psO", [al.ps([128, 512], F32, "psO") for _ in range(1)])
    fin = {k: al.sb([128, 4], F32, k) for k in ("rl", "ss", "tt", "sc")}
    Osb = al.sb([128, 260], F32, "Osb")
    OTs = al.sb([65, 512], F32, "OTs")
    idf65 = al.sb([128, 128], F32, "idf65")
    ph.add("pool", "affine_select", idf65[:], onesf[:, 0:128], [[-1, 128]], ALU.is_equal, 0.0, base=0, channel_multiplier=1,
           reads=[("onesf",)], writes=[("idf65",)])
    sqa = al.sb([128, 4, 64], F32, "sqa")
    oa = Ring("oa", [al.sb([128, 4, 64], BF16, "oa") for _ in range(2)])

    def attention():
        import os as _os
        MODE = int(_os.environ.get('K_MODE', '3'))
        for hl in range(int(_os.environ.get('K_NH', '4'))):
            k = hl % 2
            q_, k_ = QA[k], KA[k]
            qr, kr = ("QA", k), ("KA", k)
            ph.add("sp", "dma_start", out=q_[0:64, :], in_=qT[hl * 64:(hl + 1) * 64, :], chan=("q0", k), writes=[qr])
            ph.add("sp", "dma_start", out=q_[64:67, :], in_=caug[hl, 0:3, :], chan=("q1", k), reads=caug_all, writes=[qr])
            ph.add("sp", "dma_start", out=k_[0:64, :], in_=kT[hl * 64:(hl + 1) * 64, :], chan=("k0", k), writes=[kr])
            ph.add("sp", "dma_start", out=k_[67:70, :], in_=caug[hl, 3:6, :], chan=("k1", k), reads=caug_all, writes=[kr])
            NQB = int(_os.environ.get('K_NQB', str(S // 512)))
            blocks = [(qb, kb) for qb in range(qb0, NQB) for kb in range(4 * (qb + 1))]
            LA = 2
            state = {}

            def emit_S(j):
                qb, kb = blocks[j]
                diag = kb - 4 * qb
                qlo = max(0, diag) * 128
                St, Sr = psS.next()
                ph.add("pe", "matmul", St[:, qlo:512], k_[0:KR, kb * 128:(kb + 1) * 128],
                       q_[0:KR, qb * 512 + qlo:(qb + 1) * 512], start=True, stop=(diag < 0),
                       reads=[qr, kr], writes=[Sr])
                if diag >= 0:
                    ph.add("pe", "matmul", St[:, qlo:qlo + 128], ident[:, :], maskT[:, :], start=False, stop=True,
                           reads=[("ident",), ("maskT",)], writes=[Sr])
                Pt, Pr = PT.next()
                ph.add("act", "activation", Pt[:, qlo:512], St[:, qlo:512], AF.Exp, reads=[Sr], writes=[Pr])
                state[j] = (Pt, Pr)

            def emit_PV(j):
                qb, kb = blocks[j]
                diag = kb - 4 * qb
                Pt, Pr = state.pop(j)
                if kb == 0:
                    state["O"] = psO.next()
                Ot, Or_ = state["O"]
                qlo = max(0, diag) * 128
                ph.add("pe", "matmul", Ot[0:65, qlo:512], Vt[:, kb, hl, :], Pt[:, qlo:512], start=(kb == 0), stop=(kb == 4 * qb + 3),
                       reads=[Pr, ("V", hl, kb // 16)], writes=[Or_])
                if kb != 4 * qb + 3:
                    return False
                ph.add("dve", "tensor_copy", OTs[:, :], Ot[0:65, :], reads=[Or_], writes=[("OTs",)])
                Tt, Tr = psS.next()
                for s_ in range(4):
                    ph.add("pe", "transpose", Tt[:, s_ * 65:(s_ + 1) * 65], OTs[0:65, s_ * 128:(s_ + 1) * 128], idf65[0:65, 0:65],
                           reads=[("OTs",), ("idf65",)], writes=[Tr])
                rl, ss, tt, sc = fin["rl"], fin["ss"], fin["tt"], fin["sc"]
                Ot, Or_ = Tt, Tr
                ph.add("dve", "tensor_copy", Osb[:, :], Ot[:, 0:260], reads=[Or_], writes=[("Osb",)])
                Os3 = Osb[:, :].rearrange("p (s e) -> p s e", e=65)
                ph.add("dve", "reciprocal", rl[:], Os3[:, :, 64], reads=[("Osb",)], writes=[("rl",)])
                ph.add("pool", "tensor_tensor", sqa[:], Os3[:, :, 0:64], Os3[:, :, 0:64], ALU.mult, reads=[("Osb",)], writes=[("sqa",)])
                ph.add("dve", "tensor_reduce", ss[:], sqa[:], AX.X, ALU.add, reads=[("sqa",)], writes=[("ss",)])
                ph.add("dve", "tensor_tensor", tt[:], ss[:], rl[:], ALU.mult, reads=[("ss",), ("rl",)], writes=[("tt",)])
                ph.add("dve", "tensor_tensor", tt[:], tt[:], rl[:], ALU.mult, reads=[("tt",), ("rl",)], writes=[("tt",)])
                ph.add("dve", "tensor_scalar", tt[:], tt[:], 1.0 / 64.0, RMS_EPS, ALU.mult, ALU.add, reads=[("tt",)], writes=[("tt",)])
                ph.add("act", "activation", tt[:], tt[:], AF.Ln, reads=[("tt",)], writes=[("tt",)])
                ph.add("act", "activation", tt[:], tt[:], AF.Exp, scale=-0.5, reads=[("tt",)], writes=[("tt",)])
                ph.add("dve", "tensor_tensor", sc[:], tt[:], rl[:], ALU.mult, reads=[("tt",), ("rl",)], writes=[("sc",)])
                o_, or_ = oa.next()
                for s_ in range(4):
                    ph.add("dve", "scalar_tensor_tensor", o_[:, s_, :], Os3[:, s_, 0:64], sc[:, s_:s_ + 1],
                           gA[:, hl * 64:(hl + 1) * 64], ALU.mult, ALU.mult, reads=[("Osb",), ("sc",), ("gA",)], writes=[or_])
                ph.add("sp", "dma_start",
                       out=mixa[qb * 512:(qb + 1) * 512, hl * 64:(hl + 1) * 64].rearrange("(s p) e -> p s e", p=128),
                       in_=o_[:], chan=("st",) + or_, reads=[or_])
                return True

            for j in range(len(blocks) + LA):
                if j < len(blocks):
                    emit_S(j)
                if j - LA >= 0:
                    emit_PV(j - LA)
                yield

    def f32t(name, n=1, shape=(128, 512)):
        return Ring(name, [al.sb(list(shape), F32, name) for _ in range(n)])

    qsR, fR = f32t("qs", 2), f32t("ff", 2)
    vchR = Ring("vch", [al.sb([64, 8, 128], BF16, "vch") for _ in range(2)])
    gtR = Ring("gt", [al.sb([64, 8, 128], F32, "gt") for _ in range(2)])
    lg, bt, eb, dd, em, kk = [al.sb([128, 512], F32, n) for n in ("lg", "bt", "eb", "dd", "em", "kk")]
    Aq, Bm, Ab, Kd = [al.sb([128, 512], BF16, n) for n in ("Aq", "Bm", "Ab", "Kd")]
    ref, nref, ref2, ebl = [al.sb([128, 8], F32, n) for n in ("ref", "nref", "ref2", "ebl")]
    ek = al.sb([128, 512], F32, "ek")
    gR8 = [al.sb([64, 8, 128], F32, "gR8") for _ in range(2)]
    for hr_ in range(2):
        for c in range(8):
            ph.add("pool", "tensor_copy", gR8[hr_][:, c, :], gR[:, hr_ * 128:(hr_ + 1) * 128], reads=[("gR",)], writes=[("gR8", hr_)])
    S32 = al.sb([128, 128], F32, "S32")
    SbR = Ring("SbR", [al.sb([128, 128], BF16, "SbR") for _ in range(6)])
    STb = Ring("STb", [al.sb([64, 4, 64], BF16, "STb") for _ in range(2)])
    KdTs = Ring("KdTs", [al.sb([64, 4, 128], BF16, "KdTs") for _ in range(2)])
    tril4 = al.sb([64, 4, 64], F32, "tril4")
    for i4 in range(4):
        ph.add("pool", "tensor_copy", tril4[:, i4, :], tril[:], reads=[("tril",)], writes=[("tril4",)])
    ot = Ring("ot", [al.sb([64, 8, 128], F32, "ot") for _ in range(2)])
    sq = al.sb([64, 8, 128], F32, "sq")
    gg = al.sb([64, 8, 128], F32, "gg")
    ssr_, rstd = al.sb([64, 8], F32, "ssr"), al.sb([64, 8], F32, "rstd")
    yo = Ring("yo", [al.sb([64, 8, 128], BF16, "yo") for _ in range(2)])
    psSc = al.ps([128, 512], F32, "psSc")
    psOr = al.ps([128, 512], F32, "psOr")
    psK = al.ps([128, 1024], BF16, "psK")

    def hgrn():
        for hr in range(2):
            ph.add("pool", "memset", S32[:], 0.0, writes=[("S32",)])
            sb_cur = None
            loaded = {}

            def load(g):
                q_, qr = qsR.next()
                f_, fr = fR.next()
                v_, vr_ = vchR.next()
                g_, gr = gtR.next()
                ts = slice(g * 512, (g + 1) * 512)
                fs = slice(hr * 128, (hr + 1) * 128)
                ph.add("sp", "dma_start", out=q_[:], in_=qsT[fs, ts], chan=qr, writes=[qr])
                ph.add("sp", "dma_start", out=f_[:], in_=fT[fs, ts], chan=fr, writes=[fr])
                ph.add("sp", "dma_start", out=v_[:], in_=vr[ts, fs].rearrange("(c p) e -> p c e", p=64), chan=vr_, writes=[vr_])
                if g >= out0:
                    ph.add("sp", "dma_start", out=g_[:], in_=gate[ts, fs].rearrange("(c p) e -> p c e", p=64), chan=gr, writes=[gr])
                loaded[g] = (q_, qr, f_, fr, v_, vr_, g_, gr)

            load(0)
            for g in range(S // 512):
                if g + 1 < S // 512:
                    load(g + 1)
                q_, qr, f_, fr, v_, vr_, g_, gr = loaded.pop(g)
                b3 = bt[:, :].rearrange("p (c t) -> p c t", t=64)
                ph.add("act", "activation", lg[:], f_[:], AF.Ln, reads=[fr], writes=[("lg",)])
                ph.add("dve", "tensor_tensor_scan", bt[:], rmask[:], lg[:], 0.0, ALU.mult, ALU.add,
                       reads=[("lg",), ("rmask",)], writes=[("bt",)])
                ph.add("dve", "tensor_scalar", ref[:], b3[:, :, 63], 0.5, None, ALU.mult, reads=[("bt",)], writes=[("ref",)])
                full = g >= out0
                if full:
                    ph.add("act", "activation", eb[:], bt[:], AF.Exp, reads=[("bt",)], writes=[("eb",)])
                ph.add("act", "activation", ebl[:], ref[:], AF.Exp, scale=2.0, reads=[("ref",)], writes=[("ebl",)])
                ph.add("dve", "tensor_scalar", nref[:], ref[:], -1.0, None, ALU.mult, reads=[("ref",)], writes=[("nref",)])
                ph.add("dve", "tensor_scalar", ref2[:], ref[:], 2.0, None, ALU.mult, reads=[("ref",)], writes=[("ref2",)])
                for c in range(8):
                    cs = slice(c * 64, (c + 1) * 64)
                    if full:
                        ph.add("act", "activation", dd[:, cs], bt[:, cs], AF.Exp, bias=nref[:, c:c + 1], scale=1.0,
                               reads=[("bt",), ("nref",)], writes=[("ea", c)])
                        ph.add("act", "activation", em[:, cs], bt[:, cs], AF.Exp, bias=ref[:, c:c + 1], scale=-1.0,
                               reads=[("bt",), ("ref",)], writes=[("em", c)])
                    ph.add("act", "activation", ek[:, cs], bt[:, cs], AF.Exp, bias=ref2[:, c:c + 1], scale=-1.0,
                           reads=[("bt",), ("ref2",)], writes=[("ek", c)])
                ear = [("ea", c) for c in range(8)]
                emr = [("em", c) for c in range(8)]
                ekr = [("ek", c) for c in range(8)]
                ph.add("pool", "tensor_scalar", kk[:], f_[:], -1.0, 1.0, ALU.mult, ALU.add, reads=[fr], writes=[("kk",)])
                if full:
                    ph.add("dve", "tensor_tensor", Aq[:], q_[:], eb[:], ALU.mult, reads=[qr, ("eb",)], writes=[("Aq",)])
                    ph.add("dve", "tensor_tensor", Ab[:], q_[:], dd[:], ALU.mult, reads=[qr] + ear, writes=[("Ab",)])
                    ph.add("pool", "tensor_tensor", Bm[:], kk[:], em[:], ALU.mult, reads=[("kk",)] + emr, writes=[("Bm",)])
                    ph.add("pool", "tensor_tensor", gg[:], g_[:], gR8[hr][:], ALU.mult, reads=[gr, ("gR8", hr)], writes=[("gg",)])
                ph.add("pool", "tensor_tensor", Kd[:], kk[:], ek[:], ALU.mult, reads=[("kk",)] + ekr, writes=[("Kd",)])
                if full:
                    o_, or_ = ot.next()
                yield
                if flag is not None and g * 512 == HALF:
                    ph.add("dve", "tensor_scalar", S32[:], S32[:], flagt[:, 0:1], None, ALU.mult, reads=[("S32",), ("flagm",)], writes=[("S32",)])
                    sb_cur = None
                if full and sb_cur is None:
                    t_, tr_ = SbR.next()
                    ph.add("dve", "tensor_copy", t_[:], S32[:], reads=[("S32",)], writes=[tr_])
                    sb_cur = (t_, tr_)
                for hf in range(2):
                    for i4 in range(4):
                        c = 4 * hf + i4
                        cs = slice(c * 64, (c + 1) * 64)
                        ph.add("pe", "transpose", psK[0:64, i4 * 128:(i4 + 1) * 128], Kd[:, cs], ident[:, :],
                               reads=[("Kd",), ("ident",)], writes=[("psK",)])
                    kt_, ktr = KdTs.next()
                    ph.add("dve", "tensor_copy", kt_[:, :, :].rearrange("p c d -> p (c d)"), psK[0:64, 0:512],
                           reads=[("psK",)], writes=[ktr])
                    yield
                    for i4 in range(4):
                        c = 4 * hf + i4
                        ph.add("pe", "matmul", psSn[:, i4 * 128:(i4 + 1) * 128], kt_[:, i4, :], v_[:, c, :], start=True, stop=True,
                               reads=[ktr, vr_], writes=[("psSn",)])
                    yield
                    if full:
                        for i4 in range(4):
                            c = 4 * hf + i4
                            cs = slice(c * 64, (c + 1) * 64)
                            ph.add("pe", "matmul", psSc[0:64, i4 * 64:(i4 + 1) * 64], Bm[:, cs], Ab[:, cs], start=True, stop=True,
                                   reads=[("Bm",), ("Ab",)], writes=[("psSc",)])
                        st_, str_ = STb.next()
                        ph.add("dve", "scalar_tensor_tensor", st_[:, :, :].rearrange("p c t -> p (c t)"), psSc[0:64, 0:256], 3.0e38,
                               tril4[:, :, :].rearrange("p c t -> p (c t)"), ALU.min, ALU.mult,
                               reads=[("psSc",), ("tril4",)], writes=[str_])
                        yield
                    sbs = [sb_cur]
                    for i4 in range(4):
                        c = 4 * hf + i4
                        if full or (g + 1 >= out0 and c == 7):
                            t_, tr_ = SbR.next()
                            ph.add("dve", "scalar_tensor_tensor", t_[:], S32[:], ebl[:, c:c + 1], psSn[:, i4 * 128:(i4 + 1) * 128],
                                   ALU.mult, ALU.add, reads=[("S32",), ("ebl",), ("psSn",)], writes=[tr_])
                            sb_cur = (t_, tr_)
                        sbs.append(sb_cur)
                        ph.add("dve", "scalar_tensor_tensor", S32[:], S32[:], ebl[:, c:c + 1], psSn[:, i4 * 128:(i4 + 1) * 128],
                               ALU.mult, ALU.add, reads=[("S32",), ("ebl",), ("psSn",)], writes=[("S32",)])
                    yield
                    if full:
                        for i4 in range(4):
                            c = 4 * hf + i4
                            cs = slice(c * 64, (c + 1) * 64)
                            ph.add("pe", "matmul", psOr[0:64, i4 * 128:(i4 + 1) * 128], st_[:, i4, :], v_[:, c, :], start=True, stop=False,
                                   reads=[str_, vr_], writes=[("psOr",)])
                            ph.add("pe", "matmul", psOr[0:64, i4 * 128:(i4 + 1) * 128], Aq[:, cs], sbs[i4][0][:], start=False, stop=True,
                                   reads=[("Aq",), sbs[i4][1]], writes=[("psOr",)])
                        ph.add("act", "activation", o_[:, 4 * hf:4 * hf + 4, :].rearrange("p c e -> p (c e)"), psOr[0:64, 0:512], AF.Copy,
                               reads=[("psOr",)], writes=[or_ + (hf,)])
                        yield
                if not full:
                    continue
                orall = [or_ + (0,), or_ + (1,)]
                ph.add("pool", "tensor_tensor", sq[:], o_[:], o_[:], ALU.mult, reads=orall, writes=[("sq",)])
                ph.add("dve", "tensor_reduce", ssr_[:], sq[:], AX.X, ALU.add, reads=[("sq",)], writes=[("ssr",)])
                ph.add("dve", "tensor_scalar", rstd[:], ssr_[:], 1.0 / 128.0, RMS_EPS, ALU.mult, ALU.add, reads=[("ssr",)], writes=[("rstd",)])
                ph.add("act", "activation", rstd[:], rstd[:], AF.Ln, reads=[("rstd",)], writes=[("rstd",)])
                ph.add("act", "activation", rstd[:], rstd[:], AF.Exp, scale=-0.5, reads=[("rstd",)], writes=[("rstd",)])
                y_, yr = yo.next()
                for c in range(8):
                    ph.add("dve", "scalar_tensor_tensor", y_[:, c, :], o_[:, c, :], rstd[:, c:c + 1], gg[:, c, :], ALU.mult, ALU.mult,
                           reads=[or_ + (c // 4,), ("rstd",), ("gg",)], writes=[yr + (c,)])
                ph.add("sp", "dma_start",
                       out=mixr[g * 512:(g + 1) * 512, hr * 128:(hr + 1) * 128].rearrange("(c p) e -> p c e", p=64),
                       in_=y_[:], chan=("st",) + yr, reads=[yr + (c,) for c in range(8)], writes=[("mixr", hr, g)])
                yield

    import os as _os
    ga, gh = attention(), hgrn()
    alive_a, alive_h = _os.environ.get('K_ATT', '1') == '1', _os.environ.get('K_HG', '1') == '1'
    NA = 4 * sum(4 * (qb + 1) for qb in range(qb0, S // 512)) + 8
    NH = 2 * ((S // 512 - out0) * 12 + out0 * 7)
    da = dh = 0
    while alive_a or alive_h:
        pick_a = alive_a and (not alive_h or da * NH <= dh * NA)
        if pick_a:
            try:
                next(ga)
                da += 1
            except StopIteration:
                alive_a = False
        else:
            try:
                next(gh)
                dh += 1
            except StopIteration:
                alive_h = False
    ph.emit()
    al.close()


def phase_wo(cx, mix, xres, halo_mix, halo_x, w_o, lng, lnb, x1s, x1T, ntok=TOK, t0=0):
    nc, prog = cx.nc, cx.prog
    ph = Phase(prog)
    al = Alloc(nc)
    ident = make_ident(ph, al)
    stage = Ring("stg", [al.sb([128, WCH], F32, "stg") for _ in range(4)])
    Wo = load_weight_bf16(ph, al, w_o, D, D, "wo", stage)
    wall = wres("wo", 8, D, 0, D)
    gB = bcast_load(ph, al, lng, D, "lnc0")
    bB = bcast_load(ph, al, lnb, D, "lnc1")
    ph.add("pool", "engine_nop", reads=[("lnc0",), ("lnc1",)], writes=[("lnc",)])
    mt = Ring("mt", [al.sb([128, D], BF16, "mt") for _ in range(2)])
    xr = Ring("xr", [al.sb([128, D], F32, "xr") for _ in range(2)])
    yt = Ring("yt", [al.sb([128, D], F32, "yt") for _ in range(2)])
    xn = Ring("xn", [al.sb([128, D], F32, "xn") for _ in range(2)])
    xb = Ring("xb", [al.sb([128, D], BF16, "xb") for _ in range(2)])
    mT = Ring("mT", [al.sb([128, 8, 128], BF16, "mT") for _ in range(2)])
    xT = Ring("xT", [al.sb([128, 8, 128], BF16, "xT") for _ in range(2)])
    psT = Ring("psT", [al.ps([128, 1024], BF16, "psT") for _ in range(2)])
    psA = Ring("psA", [al.ps([128, 512], F32, "psA") for _ in range(6)])
    tmp = ln_tmp(al)
    x1Tv = x1T.rearrange("(kc p) t -> p kc t", p=128)
    for ti in range(t0, ntok // 128 + (1 if halo_mix is not None else 0)):
        if ti < ntok // 128:
            P, r0 = 128, ti * 128
            msrc, xsrc = mix[r0:r0 + 128, :], xres[r0:r0 + 128, :]
        else:
            P, r0 = 2, ntok
            msrc, xsrc = halo_mix[:, :], halo_x[:, :]
        m_, mr = mt.next()
        x_, xr_ = xr.next()
        ph.add("sp", "dma_start", out=m_[0:P, :], in_=msrc, chan=mr, writes=[mr])
        ph.add("sp", "dma_start", out=x_[0:P, :], in_=xsrc, chan=xr_, writes=[xr_])
        pt, ptr = psT.next()
        mT_, mTr = mT.next()
        transpose_to(ph, P, m_, mr, pt, ptr, mT_[:, :, 0:P], mTr, ident, "act")
        y_, yr = yt.next()
        for half in range(2):
            ps, psr = psA.next()
            hs = slice(half * 512, (half + 1) * 512)
            for kc in range(8):
                ph.add("pe", "matmul", ps[0:P, :], mT_[:, kc, 0:P], Wo[:, kc, hs], start=(kc == 0), stop=(kc == 7),
                       reads=wcols("wo", kc, half * 512, half * 512 + 512) + [mTr], writes=[psr])
            ph.add("dve", "scalar_tensor_tensor", y_[0:P, hs], x_[0:P, hs], ALPHA, ps[0:P, :], ALU.mult, ALU.add,
                   reads=[xr_, psr], writes=[yr])
        n_, nr = xn.next()
        layer_norm(ph, P, y_, yr, n_, nr, gB, bB, tmp, "ln")
        ph.add("sp", "dma_start", out=x1s[r0:r0 + P, :], in_=n_[0:P, :], chan=("st",) + nr, reads=[nr], writes=[("x1s", ti)])
        b_, br = xb.next()
        ph.add("act", "activation", b_[0:P, :], n_[0:P, :], AF.Copy, reads=[nr], writes=[br])
        pt, ptr = psT.next()
        xT_, xTr = xT.next()
        transpose_to(ph, P, b_, br, pt, ptr, xT_[:, :, 0:P], xTr, ident, "dve")
        ph.add("sp", "dma_start", out=x1Tv[:, :, r0:r0 + P], in_=xT_[:, :, 0:P], chan=("st",) + xTr, reads=[xTr],
               writes=[("x1T", ti)])
    ph.emit()
    al.close()


def phase_up(cx, x1T, flag, w_up, conv_w, conv_b, hgT, ntok=TOK, boundary=None, g0=0):
    nc, prog = cx.nc, cx.prog
    ph = Phase(prog)
    al = Alloc(nc)
    stage = Ring("stg", [al.sb([128, WCH], F32, "stg") for _ in range(4)])
    Wu = load_weight_bf16(ph, al, w_up, D, 2 * DFF, "wu", stage)
    wall = wres("wu", 8, 2 * DFF, 0, 2 * DFF)
    NCB = 2 * DFF // 128
    cw = al.sb([128, 3, NCB], F32, "cw")
    cb_ = al.sb([128, NCB], F32, "cb")
    ph.add("sp", "dma_start", out=cw[:], in_=conv_w.rearrange("j (cb p) -> p j cb", p=128), chan="cw", writes=[("cw",)],
           allow_slow_non_contiguous=True)
    ph.add("sp", "dma_start", out=cb_[:], in_=conv_b.rearrange("(cb p) -> p cb", p=128), chan="cb", writes=[("cb",)],
           allow_slow_non_contiguous=True)
    flagt = bcast_load(ph, al, flag, 1, "flag")
    zt = al.sb([128, 128], BF16, "zt")
    ph.add("pool", "memset", zt[:], 0.0, writes=[("zt",)])
    carry = al.sb([128, NCB, 2], F32, "carry")
    x1Tv = x1T.rearrange("(kc p) t -> p kc t", p=128)
    psA = Ring("psA", [al.ps([128, 512], F32, "psA") for _ in range(8)])
    allc = [("carry", cb) for cb in range(NCB)]
    if boundary is None:
        xh = al.sb([128, 8, 2], BF16, "xh")
        ph.add("sp", "dma_start", out=xh[:], in_=x1Tv[:, :, ntok:ntok + 2], chan="xh", writes=[("xh",)])
        psh, pshr = psA.next()
        ph.add("pe", "matmul", psh[:, 0:2 * NCB], zt[:, 0:128], zt[:, 0:2 * NCB], start=True, stop=False, reads=[("zt",)], writes=[pshr])
        for cb in range(NCB):
            for kc in range(8):
                ph.add("pe", "matmul", psh[:, 2 * cb:2 * cb + 2], Wu[:, kc, cb * 128:(cb + 1) * 128], xh[:, kc, :],
                       start=False, stop=(kc == 7), reads=wall + [("xh",)], writes=[pshr])
        ph.add("dve", "tensor_scalar", carry[:, :, :].rearrange("p c t -> p (c t)"), psh[:, 0:2 * NCB], flagt[:, 0:1], None, ALU.mult,
               reads=[pshr, ("flag",)], writes=allc)
    else:
        ph.add("pool", "memset", carry[:], 0.0, writes=allc)
    xg = Ring("xg", [al.sb([128, 8, 512], BF16, "xg") for _ in range(2)])
    tt = Ring("tt", [al.sb([128, 512], F32, "tt") for _ in range(4)])
    ga = Ring("ga", [al.sb([128, 512], F32, "ga") for _ in range(3)])
    hg = Ring("hg", [al.sb([128, NCB // 2, 512], BF16, "hg") for _ in range(1)])
    hgv = hgT.rearrange("(cb p) t -> p cb t", p=128)
    for g in range(g0, ntok // 512):
        if boundary is not None and g * 512 == boundary:
            cfl = carry[:, :, :].rearrange("p c t -> p (c t)")
            ph.add("dve", "tensor_scalar", cfl, cfl, flagt[:, 0:1], None, ALU.mult, reads=allc + [("flag",)], writes=allc)
        x_, xr_ = xg.next()
        ph.add("sp", "dma_start", out=x_[:], in_=x1Tv[:, :, g * 512:(g + 1) * 512], chan=xr_, writes=[xr_])
        h_, hr_ = hg.next()
        for j in range(NCB // 2):
            tiles = {}
            for which, cb in (("a", j), ("u", j + NCB // 2)):
                ps, psr = psA.next()
                for kc in range(8):
                    ph.add("pe", "matmul", ps[:, :], Wu[:, kc, cb * 128:(cb + 1) * 128], x_[:, kc, :], start=(kc == 0), stop=(kc == 7),
                           reads=wcols("wu", kc, cb * 128, cb * 128 + 128) + [xr_], writes=[psr])
                t_, tr = tt.next()
                ph.add("act", "activation", t_[:], ps[:, :], AF.Identity, bias=cb_[:, cb:cb + 1], scale=cw[:, 2, cb:cb + 1],
                       reads=[psr, ("cw",), ("cb",)], writes=[tr])
                ph.add("dve", "scalar_tensor_tensor", t_[:, 1:512], ps[:, 0:511], cw[:, 1, cb:cb + 1], t_[:, 1:512], ALU.mult, ALU.add,
                       reads=[psr, tr, ("cw",)], writes=[tr])
                ph.add("dve", "scalar_tensor_tensor", t_[:, 2:512], ps[:, 0:510], cw[:, 0, cb:cb + 1], t_[:, 2:512], ALU.mult, ALU.add,
                       reads=[psr, tr, ("cw",)], writes=[tr])
                ph.add("dve", "scalar_tensor_tensor", t_[:, 0:1], carry[:, cb, 1:2], cw[:, 1, cb:cb + 1], t_[:, 0:1], ALU.mult, ALU.add,
                       reads=[("carry", cb), tr, ("cw",)], writes=[tr])
                ph.add("dve", "scalar_tensor_tensor", t_[:, 0:2], carry[:, cb, 0:2], cw[:, 0, cb:cb + 1], t_[:, 0:2], ALU.mult, ALU.add,
                       reads=[("carry", cb), tr, ("cw",)], writes=[tr])
                ph.add("act", "activation", carry[:, cb, :], ps[:, 510:512], AF.Copy, reads=[psr], writes=[("carry", cb)])
                tiles[which] = (t_, tr)
            g_, gr = ga.next()
            ph.add("act", "activation", g_[:], tiles["a"][0][:], AF.Gelu, reads=[tiles["a"][1]], writes=[gr])
            ph.add("pool", "tensor_tensor", h_[:, j, :], g_[:], tiles["u"][0][:], ALU.mult, reads=[gr, tiles["u"][1]],
                   writes=[hr_ + (j,)])
        ph.add("sp", "dma_start", out=hgv[:, :, g * 512:(g + 1) * 512], in_=h_[:], chan=("st",) + hr_,
               reads=[hr_ + (j,) for j in range(NCB // 2)], writes=[("hgT", g)])
    ph.emit()
    al.close()


def phase_down(cx, hgT, x1s, w_down, lng, lnb, xout, ntok=TOK, g0=0):
    nc, prog = cx.nc, cx.prog
    ph = Phase(prog)
    al = Alloc(nc)
    stage = Ring("stg", [al.sb([128, WCH], F32, "stg") for _ in range(4)])
    NCB = DFF // 128
    Wd = load_weight_bf16(ph, al, w_down, DFF, D, "wd", stage)
    wall = wres("wd", NCB, D, 0, D)
    gB = bcast_load(ph, al, lng, D, "lnc0")
    bB = bcast_load(ph, al, lnb, D, "lnc1")
    ph.add("pool", "engine_nop", reads=[("lnc0",), ("lnc1",)], writes=[("lnc",)])
    hg = Ring("hg", [al.sb([128, NCB, 512], BF16, "hg") for _ in range(2)])
    xr = Ring("xr", [al.sb([128, D], F32, "xr") for _ in range(2)])
    yt = Ring("yt", [al.sb([128, D], F32, "yt") for _ in range(2)])
    xn = Ring("xn", [al.sb([128, D], F32, "xn") for _ in range(2)])
    psA = Ring("psA", [al.ps([128, 512], F32, "psA") for _ in range(8)])
    tmp = ln_tmp(al)
    hgv = hgT.rearrange("(cb p) t -> p cb t", p=128)
    for g in range(g0, ntok // 512):
        h_, hr_ = hg.next()
        ph.add("sp", "dma_start", out=h_[:], in_=hgv[:, :, g * 512:(g + 1) * 512], chan=hr_, writes=[hr_])
        for t in range(4):
            r0 = g * 512 + t * 128
            x_, xr_ = xr.next()
            ph.add("sp", "dma_start", out=x_[:], in_=x1s[r0:r0 + 128, :], chan=xr_, writes=[xr_])
            y_, yr = yt.next()
            for half in range(2):
                ps, psr = psA.next()
                hs = slice(half * 512, (half + 1) * 512)
                for cb in range(NCB):
                    ph.add("pe", "matmul", ps[:, :], h_[:, cb, t * 128:(t + 1) * 128], Wd[:, cb, hs], start=(cb == 0), stop=(cb == NCB - 1),
                           reads=wcols("wd", cb, half * 512, half * 512 + 512) + [hr_], writes=[psr])
                ph.add("dve", "scalar_tensor_tensor", y_[:, hs], x_[:, hs], ALPHA, ps[:, :], ALU.mult, ALU.add,
                       reads=[xr_, psr], writes=[yr])
            n_, nr = xn.next()
            layer_norm(ph, 128, y_, yr, n_, nr, gB, bB, tmp, "ln")
            ph.add("sp", "dma_start", out=xout[r0:r0 + 128, :], in_=n_[:], chan=("st",) + nr, reads=[nr], writes=[("xout", r0)])
    ph.emit()
    al.close()


def build_launch3(with_proj):
    cx = Ctx()
    mix = cx.din("mix", [TOK, D], BF16)
    xres = cx.din("xres", [TOK, D])
    halo_mix = cx.din("halo_mix", [2, D], BF16)
    halo_x = cx.din("halo_x", [2, D])
    flag = cx.din("flag", [1])
    w_o = cx.din("w_o", [D, D])
    lnm_g, lnm_b = cx.din("lnm_g", [D]), cx.din("lnm_b", [D])
    w_up = cx.din("w_up", [D, 2 * DFF])
    conv_w, conv_b = cx.din("conv_w", [3, 2 * DFF]), cx.din("conv_b", [2 * DFF])
    w_down = cx.din("w_down", [DFF, D])
    lnf_g, lnf_b = cx.din("lnf_g", [D]), cx.din("lnf_b", [D])
    xout = cx.dout("xout", [TOK, D])
    x1s = cx.dscr("x1s", [TOK + 2, D], F32)
    x1T = cx.dscr("x1T", [D, TOK + 2], BF16)
    hgT = cx.dscr("hgT", [DFF, TOK], BF16)
    if with_proj:
        w_in = cx.din("w_in", [D, INC])
        fb = cx.din("fbias", [8])
        lbnd = cx.din("lbnd", [2, 512])
        outs = proj_outputs(cx)
    phase_wo(cx, mix, xres, halo_mix, halo_x, w_o, lnm_g, lnm_b, x1s, x1T)
    phase_up(cx, x1T, flag, w_up, conv_w, conv_b, hgT)
    phase_down(cx, hgT, x1s, w_down, lnf_g, lnf_b, xout)
    if with_proj:
        phase_proj(cx, xout, None, False, None, None, w_in, fb, lbnd, 1, outs)
    return cx


def _c(a):
    return np.ascontiguousarray(a)


def _run(cx, in_maps):
    res = run_bass_kernel_spmd(cx.nc, in_maps, core_ids=list(range(8)))
    return res.results


def _mixer_inputs(pr, l, fox_norm_g, hgrn_norm_g):
    ins = []
    for c in range(8):
        b, hg = c // 2, c % 2
        lo, hi = pr[2 * b], pr[2 * b + 1]
        a = slice(hg * 256, (hg + 1) * 256)
        ins.append({
            "qT": _c(np.concatenate([lo["qT"][a], hi["qT"][a]], axis=1)),
            "kT": _c(np.concatenate([lo["kT"][a], hi["kT"][a]], axis=1)),
            "v": _c(np.concatenate([lo["v"][:, a], hi["v"][:, a]], axis=0)),
            "lf": _c(np.concatenate([lo["lf"][hg * 4:(hg + 1) * 4], hi["lf"][hg * 4:(hg + 1) * 4]], axis=1)),
            "qsT": _c(np.concatenate([lo["qsT"][a], hi["qsT"][a]], axis=1)),
            "fT": _c(np.concatenate([lo["fT"][a], hi["fT"][a]], axis=1)),
            "vr": _c(np.concatenate([lo["vr"][:, a], hi["vr"][:, a]], axis=0)),
            "gate": _c(np.concatenate([lo["gate"][:, a], hi["gate"][:, a]], axis=0)),
            "gfox": _c(fox_norm_g[l, a]),
            "ghg": _c(hgrn_norm_g[l, a]),
        })
    return ins


def _ffn_inputs(mx, xres, l, P, with_proj):
    ins = []
    for c in range(8):
        b, h = c // 2, c % 2
        m0, m1 = mx[2 * b], mx[2 * b + 1]
        mixfull = np.concatenate([m0["mixa"], m1["mixa"], m0["mixr"], m1["mixr"]], axis=1)
        sl = slice(h * TOK, (h + 1) * TOK)
        if h == 0:
            hm = np.zeros((2, D), mixfull.dtype)
            hx = np.zeros((2, D), np.float32)
            fl = np.zeros((1,), np.float32)
        else:
            hm = mixfull[TOK - 2:TOK]
            hx = xres[2 * b][TOK - 2:TOK]
            fl = np.ones((1,), np.float32)
        d = {"mix": _c(mixfull[sl]), "xres": _c(xres[c]), "halo_mix": _c(hm), "halo_x": _c(hx), "flag": fl,
             "w_o": _c(P["w_o"][l]), "lnm_g": _c(P["ln_mix_g"][l]), "lnm_b": _c(P["ln_mix_b"][l]),
             "w_up": _c(P["w_up"][l]), "conv_w": _c(P["conv_w"][l]), "conv_b": _c(P["conv_b"][l]),
             "w_down": _c(P["w_down"][l]), "lnf_g": _c(P["ln_ffn_g"][l]), "lnf_b": _c(P["ln_ffn_b"][l])}
        if with_proj:
            d.update({"w_in": _c(P["w_in"][l + 1]), "fbias": _c(P["fox_f_bias"][l + 1]), "lbnd": _c(P["hgrn_lower_bounds"])})
        ins.append(d)
    return ins


def build_fused():
    cx = Ctx()
    x_all = cx.din("x_all", [S, D])
    flag = cx.din("flag", [1])
    ln_emb_g, ln_emb_b = cx.din("ln_emb_g", [D]), cx.din("ln_emb_b", [D])
    w_in = cx.din("w_in", [DEPTH, D, INC])
    fbias = cx.din("fox_f_bias", [DEPTH, 8])
    gfox = cx.din("fox_norm_g", [DEPTH, 512])
    lbnd = cx.din("hgrn_lower_bounds", [DEPTH, 512])
    ghg = cx.din("hgrn_norm_g", [DEPTH, 512])
    w_o = cx.din("w_o", [DEPTH, D, D])
    lnm_g, lnm_b = cx.din("ln_mix_g", [DEPTH, D]), cx.din("ln_mix_b", [DEPTH, D])
    w_up = cx.din("w_up", [DEPTH, D, 2 * DFF])
    conv_w, conv_b = cx.din("conv_w", [DEPTH, 3, 2 * DFF]), cx.din("conv_b", [DEPTH, 2 * DFF])
    w_down = cx.din("w_down", [DEPTH, DFF, D])
    lnf_g, lnf_b = cx.din("ln_ffn_g", [DEPTH, D]), cx.din("ln_ffn_b", [DEPTH, D])
    xfin = cx.dout("xfin", [S, D])
    xres = cx.dscr("xres", [S, D])
    xmid = cx.dscr("xmid", [S, D])
    outs = (cx.dscr("qT", [512, S], BF16), cx.dscr("kT", [512, S], BF16), cx.dscr("v", [S, 512], BF16),
            cx.dscr("lf", [8, S], F32), cx.dscr("qsT", [512, S], F32), cx.dscr("fT", [512, S], F32),
            cx.dscr("vr", [S, 512], BF16), cx.dscr("gate", [S, 512], F32))
    qT, kT, v, lf, qsT, fT, vr, gate = outs
    mix = cx.dscr("mix", [S, D], BF16)
    x1s = cx.dscr("x1s", [S, D], F32)
    x1T = cx.dscr("x1T", [D, S], BF16)
    hgT = cx.dscr("hgT", [DFF, S], BF16)
    caug = [cx.dscr("caug%d" % i, [8, 6, S], BF16) for i in range(DEPTH)]
    mrows = [cx.dscr("mrow%d" % i, [2, S], BF16) for i in range(2 * DEPTH)]
    for l in range(DEPTH):
        last = (l == DEPTH - 1)
        import os as _os
        TRIM = _os.environ.get('K_TRIM', '1') == '1'
        CAUGP = _os.environ.get('K_CAUGP', '0') == '1'
        G7 = (S // 2) // 512 - 1 if (last and TRIM) else 0
        if l == 0:
            phase_proj(cx, x_all, xres, True, ln_emb_g, ln_emb_b, w_in[0], fbias[0], lbnd, 0, outs, ntok=S, caug=caug[0] if CAUGP else None,
                       full_from=G7)
            xin = xres
        else:
            phase_proj(cx, xmid, None, False, None, None, w_in[l], fbias[l], lbnd, l, outs, ntok=S, caug=caug[l] if CAUGP else None, full_from=G7)
            xin = xmid
        for hg in range(2):
            a = slice(hg * 256, (hg + 1) * 256)
            mixer_phase(cx, qT[a, :], kT[a, :], v[:, a], lf[hg * 4:(hg + 1) * 4, :], qsT[a, :], fT[a, :], vr[:, a], gate[:, a],
                        gfox[l, a], ghg[l, a], mix[:, hg * 256:(hg + 1) * 256], mix[:, 512 + hg * 256:512 + (hg + 1) * 256],
                        caug[l][hg * 4:(hg + 1) * 4], flag=flag, caug_ready=CAUGP, qb0=G7, out0=G7, mrow=mrows[2 * l + hg])
        phase_wo(cx, mix, xin, None, None, w_o[l], lnm_g[l], lnm_b[l], x1s, x1T, ntok=S, t0=G7 * 4)
        phase_up(cx, x1T, flag, w_up[l], conv_w[l], conv_b[l], hgT, ntok=S, boundary=S // 2, g0=G7)
        phase_down(cx, hgT, x1s, w_down[l], lnf_g[l], lnf_b[l], xmid if not last else xfin, ntok=S,
                   g0=(S // 2) // 512 if (last and TRIM) else 0)
    return cx


def kernel(**inputs):
    P = {k: np.asarray(v) for k, v in inputs.items()}
    x = P["x"]
    ins = []
    for c in range(8):
        b, h = c // 2, c % 2
        d = {k: _c(P[k]) for k in P if k != "x"}
        d["x_all"] = _c(np.concatenate([x[b, 0:TOK], x[b, h * TOK:(h + 1) * TOK]], axis=0))
        d["flag"] = np.full((1,), float(h), np.float32)
        ins.append(d)
    res = _run(build_fused(), ins)
    out = np.empty((NB, S, D), np.float32)
    for c in range(8):
        b, h = c // 2, c % 2
        out[b, h * TOK:(h + 1) * TOK] = res[c]["xfin"][TOK:]
    return out


def kernel_unfused(**inputs):
    P = {k: np.asarray(v) for k, v in inputs.items()}
    x = P["x"]
    ins = []
    for c in range(8):
        b, h = c // 2, c % 2
        ins.append({"x": _c(x[b, h * TOK:(h + 1) * TOK]), "ln_g": _c(P["ln_emb_g"]), "ln_b": _c(P["ln_emb_b"]),
                    "w_in": _c(P["w_in"][0]), "fbias": _c(P["fox_f_bias"][0]), "lbnd": _c(P["hgrn_lower_bounds"])})
    pr = _run(build_launch1(), ins)
    xres = [r["xres"] for r in pr]
    out = None
    for l in range(DEPTH):
        mx = _run(build_launch2(), _mixer_inputs(pr, l, P["fox_norm_g"], P["hgrn_norm_g"]))
        last = (l == DEPTH - 1)
        pr = _run(build_launch3(not last), _ffn_inputs(mx, xres, l, P, not last))
        xres = [r["xout"] for r in pr]
    out = np.empty((NB, S, D), np.float32)
    for c in range(8):
        b, h = c // 2, c % 2
        out[b, h * TOK:(h + 1) * TOK] = xres[c]
    return out


kernel_fused = kernel
kernel = kernel_unfused
```
